# Optimizing a Trainium2 kernel written in Bass

```python
import math
import jax, jax.numpy as jnp
from jax import lax
import numpy as np

D_MODEL = 2048
BATCH = 2
SEQ = 8192
DEPTH = 2

N_MIXERS = 2
N_A_LAYERS = (DEPTH + 1) // 2
N_B_LAYERS = DEPTH // 2
EPS = 1e-6

M_HEADS = 8
M_QK_DIM = D_MODEL // 2
M_V_DIM = D_MODEL
M_DK = M_QK_DIM // M_HEADS
M_DV = M_V_DIM // M_HEADS
M_PROJ = 2 * M_QK_DIM + 2 * M_V_DIM + 2 * M_HEADS
M_CHUNK = 64

R_WIDTH = D_MODEL
R_BLOCKS = 8
R_BLOCK_W = R_WIDTH // R_BLOCKS
R_CONV_W = 4
R_C = 8.0

D_FF = ((8 * D_MODEL // 3 + 255) // 256) * 256

kernel_name = "hybrid_mlstm_rglru_interleaved"


def rms_norm(x, g):
    xf = x.astype(jnp.float32)
    y = xf * lax.rsqrt(jnp.mean(xf * xf, axis=-1, keepdims=True) + EPS)
    return (y * g.astype(jnp.float32)).astype(x.dtype)


def swiglu(x, w_in, w_out):
    g, u = jnp.split(x @ w_in, 2, axis=-1)
    return (jax.nn.silu(g) * u) @ w_out


def mlstm_chunkwise(q, k, v, ig, lf):
    B, H, S, DK = q.shape
    DV = v.shape[-1]
    nc = S // M_CHUNK

    def to_chunks(t):
        return jnp.moveaxis(t.reshape(B, H, nc, M_CHUNK, *t.shape[3:]), 2, 0)

    qc, kc, vc, ic, fc = (to_chunks(t) for t in (q, k, v, ig, lf))
    causal = jnp.tril(jnp.ones((M_CHUNK, M_CHUNK), dtype=bool))

    def step(carry, xs):
        C, n, m = carry
        qb, kb, vb, ib, fb = xs
        b = jnp.cumsum(fb, axis=-1)
        dmat = b[..., :, None] - b[..., None, :] + ib[..., None, :]
        dmat = jnp.where(causal, dmat, -jnp.inf)
        m_inter = b + m[..., None]
        m_t = jnp.maximum(m_inter, jnp.max(dmat, axis=-1))
        w = jnp.exp(dmat - m_t[..., None])
        s = jnp.einsum('bhtd,bhsd->bhts', qb, kb) * w
        scale_inter = jnp.exp(m_inter - m_t)
        num = (jnp.einsum('bhts,bhsv->bhtv', s, vb)
               + scale_inter[..., None] * jnp.einsum('bhtd,bhdv->bhtv', qb, C))
        den = jnp.sum(s, axis=-1) + scale_inter * jnp.einsum('bhtd,bhd->bht', qb, n)
        h = num / jnp.maximum(jnp.abs(den), jnp.exp(-m_t))[..., None]
        b_last = b[..., -1]
        g = b_last[..., None] - b + ib
        m_new = jnp.maximum(b_last + m, jnp.max(g, axis=-1))
        wk = jnp.exp(g - m_new[..., None])
        decay = jnp.exp(b_last + m - m_new)
        kw = kb * wk[..., None]
        C_new = decay[..., None, None] * C + jnp.einsum('bhsd,bhsv->bhdv', kw, vb)
        n_new = decay[..., None] * n + jnp.sum(kw, axis=-2)
        return (C_new, n_new, m_new), h

    init = (jnp.zeros((B, H, DK, DV), jnp.float32),
            jnp.zeros((B, H, DK), jnp.float32),
            jnp.zeros((B, H), jnp.float32))
    _, hc = lax.scan(step, init, (qc, kc, vc, ic, fc))
    return jnp.moveaxis(hc, 0, 2).reshape(B, H, S, DV)


def mlstm_mixer(x, w_in, b_if, head_norm, w_out):
    B, S, _ = x.shape
    proj = x @ w_in
    q, k, v, o, if_pre = jnp.split(
        proj, [M_QK_DIM, 2 * M_QK_DIM, 2 * M_QK_DIM + M_V_DIM, 2 * M_QK_DIM + 2 * M_V_DIM], axis=-1)

    def heads(t, d):
        return t.reshape(B, S, M_HEADS, d).transpose(0, 2, 1, 3).astype(jnp.float32)

    qh = heads(q, M_DK)
    kh = heads(k, M_DK) * (M_DK ** -0.5)
    vh = heads(v, M_DV)
    gates = (if_pre + b_if).astype(jnp.float32).reshape(B, S, 2, M_HEADS)
    ig = gates[:, :, 0].transpose(0, 2, 1)
    lf = jax.nn.log_sigmoid(gates[:, :, 1]).transpose(0, 2, 1)
    h = mlstm_chunkwise(qh, kh, vh, ig, lf)
    h = h * lax.rsqrt(jnp.mean(h * h, axis=-1, keepdims=True) + EPS)
    h = h.transpose(0, 2, 1, 3).reshape(B, S, M_V_DIM) * head_norm.astype(jnp.float32)
    h = h * jax.nn.sigmoid(o.astype(jnp.float32))
    return h.astype(x.dtype) @ w_out


def causal_depthwise_conv(x, w, b):
    C = x.shape[-1]
    y = lax.conv_general_dilated(
        x, w[:, None, :].astype(x.dtype), window_strides=(1,),
        padding=[(R_CONV_W - 1, 0)], dimension_numbers=('NWC', 'WIO', 'NWC'),
        feature_group_count=C)
    return y + b


def rglru_mixer(x, w_in, conv_w, conv_b, gate_w, gate_b, a_param, w_out):
    B, S, _ = x.shape
    gate_branch, rec = jnp.split(x @ w_in, 2, axis=-1)
    rec = causal_depthwise_conv(rec, conv_w, conv_b)
    xb = rec.reshape(B, S, R_BLOCKS, R_BLOCK_W)
    gates = jnp.einsum('bsgi,gio->bsgo', xb, gate_w) + gate_b
    r_pre, i_pre = jnp.split(gates.astype(jnp.float32), 2, axis=-1)
    r = jax.nn.sigmoid(r_pre).reshape(B, S, R_WIDTH)
    i = jax.nn.sigmoid(i_pre).reshape(B, S, R_WIDTH)
    log_a = R_C * r * jax.nn.log_sigmoid(a_param.astype(jnp.float32))
    a = jnp.exp(log_a)
    mult = jnp.sqrt(-jnp.expm1(2.0 * log_a))
    u = mult * (i * rec.astype(jnp.float32))

    def combine(left, right):
        a1, b1 = left
        a2, b2 = right
        return a2 * a1, a2 * b1 + b2

    _, h = lax.associative_scan(combine, (a, u), axis=1)
    y = jax.nn.gelu(gate_branch.astype(jnp.float32)) * h
    return y.astype(x.dtype) @ w_out


def setup_inputs(seed: int = 0) -> dict:
    key = jax.random.key(seed)
    ks = jax.random.split(key, 20)
    f32 = jnp.float32

    def nrm(k, shape, scale):
        return jax.random.normal(k, shape, f32) * scale

    x = jax.random.normal(ks[0], (BATCH, SEQ, D_MODEL), f32)
    norm_mix = 1.0 + nrm(ks[1], (DEPTH, D_MODEL), 0.02)
    norm_ffn = 1.0 + nrm(ks[2], (DEPTH, D_MODEL), 0.02)
    norm_final = 1.0 + nrm(ks[3], (D_MODEL,), 0.02)

    m_w_in = nrm(ks[4], (N_A_LAYERS, D_MODEL, M_PROJ), D_MODEL ** -0.5)
    kb1, kb2 = jax.random.split(ks[5])
    m_b_if = jnp.concatenate([nrm(kb1, (N_A_LAYERS, M_HEADS), 0.1),
                              3.0 + nrm(kb2, (N_A_LAYERS, M_HEADS), 0.1)], axis=-1)
    m_head_norm = 1.0 + nrm(ks[6], (N_A_LAYERS, M_V_DIM), 0.02)
    m_w_out = nrm(ks[7], (N_A_LAYERS, M_V_DIM, D_MODEL), M_V_DIM ** -0.5)

    r_w_in = nrm(ks[8], (N_B_LAYERS, D_MODEL, 2 * R_WIDTH), D_MODEL ** -0.5)
    r_conv_w = nrm(ks[9], (N_B_LAYERS, R_CONV_W, R_WIDTH), R_CONV_W ** -0.5)
    r_conv_b = nrm(ks[10], (N_B_LAYERS, R_WIDTH), 0.02)
    r_gate_w = nrm(ks[11], (N_B_LAYERS, R_BLOCKS, R_BLOCK_W, 2 * R_BLOCK_W), R_BLOCK_W ** -0.5)
    r_gate_b = nrm(ks[12], (N_B_LAYERS, R_BLOCKS, 2 * R_BLOCK_W), 0.02)
    a_c = jax.random.uniform(ks[13], (N_B_LAYERS, R_WIDTH), f32, 0.9, 0.999)
    a_base = a_c ** (1.0 / R_C)
    r_a_param = jnp.log(a_base) - jnp.log1p(-a_base)
    r_w_out = nrm(ks[14], (N_B_LAYERS, R_WIDTH, D_MODEL), R_WIDTH ** -0.5)

    ffn_w_in = nrm(ks[15], (DEPTH, D_MODEL, 2 * D_FF), D_MODEL ** -0.5)
    ffn_w_out = nrm(ks[16], (DEPTH, D_FF, D_MODEL), D_FF ** -0.5)

    return {"x": x, "norm_mix": norm_mix, "norm_ffn": norm_ffn, "norm_final": norm_final,
            "m_w_in": m_w_in, "m_b_if": m_b_if, "m_head_norm": m_head_norm, "m_w_out": m_w_out,
            "r_w_in": r_w_in, "r_conv_w": r_conv_w, "r_conv_b": r_conv_b, "r_gate_w": r_gate_w,
            "r_gate_b": r_gate_b, "r_a_param": r_a_param, "r_w_out": r_w_out,
            "ffn_w_in": ffn_w_in, "ffn_w_out": ffn_w_out}


def reference(x, norm_mix, norm_ffn, norm_final, m_w_in, m_b_if, m_head_norm, m_w_out,
              r_w_in, r_conv_w, r_conv_b, r_gate_w, r_gate_b, r_a_param, r_w_out,
              ffn_w_in, ffn_w_out):
    h = x
    for layer in range(DEPTH):
        hn = rms_norm(h, norm_mix[layer])
        j = layer // N_MIXERS
        if layer % N_MIXERS == 0:
            mix = mlstm_mixer(hn, m_w_in[j], m_b_if[j], m_head_norm[j], m_w_out[j])
        else:
            mix = rglru_mixer(hn, r_w_in[j], r_conv_w[j], r_conv_b[j], r_gate_w[j],
                              r_gate_b[j], r_a_param[j], r_w_out[j])
        h = h + mix
        h = h + swiglu(rms_norm(h, norm_ffn[layer]), ffn_w_in[layer], ffn_w_out[layer])
    return rms_norm(h, norm_final)
```

```python
import numpy as np
import concourse.bass as bass
import concourse.mybir as mybir
from concourse.bass_utils import run_bass_kernel_spmd
from contextlib import ExitStack

F32 = mybir.dt.float32
BF16 = mybir.dt.bfloat16
AF = mybir.ActivationFunctionType
ALU = mybir.AluOpType

ENGS = ("pe", "act", "dve", "pool", "sp")
DMA_K = {"sp": 3, "pool": 4}

D = 2048
KC = 16
DFF = 5632
FCH = 44
GRP = 11
NGRP = 4
EPS = 1e-6
NCORES = 8
SW = 2064


class Res:
    __slots__ = ("writer", "readers", "dreaders")

    def __init__(self):
        self.writer = None
        self.readers = {}
        self.dreaders = []


class Op:
    __slots__ = ("eng", "fn", "deps", "signal", "semval", "dma", "dma_sem", "dma_val", "pre")

    def __init__(self, eng, fn, dma=False):
        self.eng = eng
        self.fn = fn
        self.deps = []
        self.signal = False
        self.semval = None
        self.dma = dma
        self.dma_sem = None
        self.dma_val = None
        self.pre = None


class Prog:
    def __init__(self, nc):
        self.nc = nc
        self.streams = {e: [] for e in ENGS}
        self.dma_count = {e: 0 for e in DMA_K}
        self.dma_hist = {e: [] for e in DMA_K}
        self.stack = ExitStack()
        self.last = {}
        self.colls = []

    def sbuf(self, name, shape, dtype):
        return self.stack.enter_context(self.nc.sbuf_tensor(name, list(shape), dtype))

    def psum(self, name, shape, dtype=F32):
        return self.stack.enter_context(self.nc.psum_tensor(name, list(shape), dtype))

    def _add(self, op, reads, writes, extra=()):
        deps = []
        rawset = set()
        for r in reads:
            if r.writer is not None:
                deps.append(r.writer)
                rawset.add(id(r.writer))
        for w in writes:
            if w.writer is not None:
                deps.append(w.writer)
            deps.extend(w.readers.values())
            deps.extend(w.dreaders)
        seen = set()
        for d in list(deps) + list(extra):
            if id(d) in seen or d is op:
                continue
            seen.add(id(d))
            if d.eng == op.eng and not d.dma and not op.dma and d not in extra:
                if id(d) not in rawset:
                    continue
            op.deps.append(d)
            if not d.dma:
                d.signal = True
        for r in reads:
            if op.dma:
                r.dreaders.append(op)
            else:
                r.readers[op.eng] = op
        for w in writes:
            w.writer = op
            w.readers = {}
            w.dreaders = []
        self.streams[op.eng].append(op)
        if op.fn is not None and not op.dma:
            self.last[op.eng] = op
        return op

    def op(self, eng, fn, reads=(), writes=(), extra=()):
        return self._add(Op(eng, fn), list(reads), list(writes), extra)

    def dma(self, eng, out, in_, reads=(), writes=()):
        op = Op(eng, lambda e: e.dma_start(out=out, in_=in_), dma=True)
        i = self.dma_count[eng]
        self.dma_count[eng] += 1
        K = DMA_K[eng]
        op.dma_sem = (eng, i % K)
        op.dma_val = (i // K + 1) * 16
        if i >= K:
            op.pre = self.dma_hist[eng][i - K]
        self.dma_hist[eng].append(op)
        return self._add(op, list(reads), list(writes))

    def coll(self, fn, reads=(), writes=()):
        op = Op("pool", fn, dma=True)
        op.dma_sem = ("cc", len(self.colls))
        op.dma_val = 1
        self.colls.append(op)
        return self._add(op, list(reads), list(writes))

    def barrier(self):
        outstanding = list(self.colls)
        for e, K in DMA_K.items():
            outstanding.extend(self.dma_hist[e][-K:])
        lasts = dict(self.last)
        for e in ENGS:
            extra = [o for (e2, o) in lasts.items() if e2 != e] + outstanding
            self._add(Op(e, None), [], [], extra)

    def emit(self, final_waits=()):
        nc = self.nc
        st = self.stack
        csem = {e: st.enter_context(nc.semaphore("c_" + e)) for e in ("pe", "act", "dve", "pool")}
        dsem = {}
        for e, K in DMA_K.items():
            for k in range(K):
                dsem[(e, k)] = st.enter_context(nc.semaphore("d_%s%d" % (e, k)))
        for k in range(len(self.colls)):
            dsem[("cc", k)] = st.enter_context(nc.semaphore("cc%d" % k))
        for e in ENGS:
            c = 0
            for op in self.streams[e]:
                if op.signal and not op.dma:
                    c += 1
                    op.semval = c
        block = st.enter_context(nc.Block())

        def handle(op):
            if op.dma:
                return dsem[op.dma_sem], op.dma_val
            return csem[op.eng], op.semval

        def run(e, eng):
            waited = {}
            for op in self.streams[e]:
                ws = [handle(d) for d in op.deps]
                if op.pre is not None:
                    ws.append(handle(op.pre))
                for (s, v) in ws:
                    if waited.get(id(s), 0) >= v:
                        continue
                    waited[id(s)] = v
                    eng.wait_ge(s, v)
                if op.fn is None:
                    continue
                ins = op.fn(eng)
                if op.dma:
                    ins.then_inc(dsem[op.dma_sem], 1 if op.dma_sem[0] == "cc" else 16)
                elif op.signal:
                    ins.then_inc(csem[e], 1)
            if e == "sp":
                for op in final_waits:
                    s, v = handle(op)
                    eng.wait_ge(s, v)

        @block.tensor
        def _(eng):
            run("pe", eng)

        @block.scalar
        def _(eng):
            run("act", eng)

        @block.vector
        def _(eng):
            run("dve", eng)

        @block.gpsimd
        def _(eng):
            run("pool", eng)

        @block.sync
        def _(eng):
            run("sp", eng)

    def close(self):
        self.stack.close()


PHASES = ["P1", "P2", "F0", "P3", "P4", "F1"]
LAUNCHES = {"L1": ["P1"], "L2": ["P2", "F0"], "L3": ["P3"], "L4": ["P4", "F1"], "ALL": list(PHASES)}

V_NM0, V_NM1, V_NF0, V_NF1, V_NFIN, V_HNG, V_CW, V_CB, V_GBR, V_GBI, V_AP = 0, 16, 32, 48, 64, 80, 96, 160, 176, 192, 208
NV = 224
ARENA = 51800


class KB:
    def __init__(self, launch, tok=2048):
        self.launch = launch
        self.ph = LAUNCHES[launch]
        self.tok = tok
        self.nt = tok // 512
        self.nc = bass.Bass("TRN2", target_bir_lowering=False)
        self.P = Prog(self.nc)
        self.outs = []
        self.build()

    def ext_in(self, name, shape, dtype=F32):
        return self.nc.dram_tensor(name, list(shape), dtype, kind="ExternalInput").ap()

    def handoff(self, name, shape, prod, cons, dtype=F32):
        pin = prod in self.ph
        cin = [c for c in cons if c in self.ph]
        cout = [c for c in cons if c not in self.ph and not (self.launch == "ALL")]
        if pin and not cout:
            kind = "Internal"
        elif pin:
            kind = "ExternalOutput"
        elif cin:
            kind = "ExternalInput"
        else:
            return None
        t = self.nc.dram_tensor(name, list(shape), dtype, kind=kind).ap()
        if kind == "ExternalOutput":
            self.outnames.append(name)
        return t

    def areset(self):
        self.aoff = 0

    def af32(self, shape):
        n = int(np.prod(shape[1:]))
        v = self.arena[:, self.aoff:self.aoff + n]
        self.aoff += n
        assert self.aoff <= ARENA, ("arena overflow", self.aoff)
        if len(shape) == 3:
            v = v.rearrange("p (a b) -> p a b", a=shape[1])
        return v

    def abf(self, shape):
        n = int(np.prod(shape[1:]))
        assert n % 2 == 0
        v = self.arena[:, self.aoff:self.aoff + n // 2].bitcast(BF16)
        self.aoff += n // 2
        assert self.aoff <= ARENA, ("arena overflow", self.aoff)
        if len(shape) == 3:
            v = v.rearrange("p (a b) -> p a b", a=shape[1])
        return v

    def mm(self, out, lhsT, rhs, start, stop, reads=(), writes=()):
        return self.P.op("pe", lambda e: e.matmul(out, lhsT=lhsT, rhs=rhs, start=start, stop=stop), reads, writes)

    def act(self, out, in_, func, reads=(), writes=(), **kw):
        return self.P.op("act", lambda e: e.activation(out=out, in_=in_, func=func, **kw), reads, writes)

    def tt(self, eng, out, in0, in1, op, reads=(), writes=()):
        return self.P.op(eng, lambda e: e.tensor_tensor(out=out, in0=in0, in1=in1, op=op), reads, writes)

    def ts(self, eng, out, in0, s1, s2, op0, op1, reads=(), writes=()):
        return self.P.op(eng, lambda e: e.tensor_scalar(out=out, in0=in0, scalar1=s1, scalar2=s2, op0=op0, op1=op1), reads, writes)

    def stt(self, eng, out, in0, scalar, in1, op0, op1, reads=(), writes=()):
        return self.P.op(eng, lambda e: e.scalar_tensor_tensor(out=out, in0=in0, scalar=scalar, in1=in1, op0=op0, op1=op1), reads, writes)

    def cp(self, eng, out, in_, reads=(), writes=()):
        if eng == "act":
            return self.P.op("act", lambda e: e.copy(out=out, in_=in_), reads, writes)
        return self.P.op(eng, lambda e: e.tensor_copy(out=out, in_=in_), reads, writes)

    def wtile(self):
        i = self.wt_i % len(self.WT)
        self.wt_i += 1
        return self.WT[i], self.rWT[i]

    def load_w(self, wview, c0, ncols):
        buf, r = self.wtile()
        self.P.dma("pool", buf[:, :, 0:ncols], wview[:, :, c0:c0 + ncols], writes=[r])
        return buf, r

    def build(self):
        nc, P, tok = self.nc, self.P, self.tok
        ph = self.ph
        self.outnames = []
        if "P1" in ph or "P2" in ph:
            self.xT = self.ext_in("xT", [128, KC, tok])
            self.m_w_in = self.ext_in("m_w_in", [D, 6160]).rearrange("(kc p) n -> p kc n", p=128)
            self.m_b_if = self.ext_in("m_b_if", [16])
        if "P2" in ph:
            self.m_w_out = self.ext_in("m_w_out", [D, D]).rearrange("(kc p) n -> p kc n", p=128)
        if "P3" in ph:
            self.r_w_in = self.ext_in("r_w_in", [D, 2 * D]).rearrange("(kc p) n -> p kc n", p=128)
            self.r_gate_w = self.ext_in("r_gate_w", [8, 256, 512]).rearrange("g (ic p) o -> p g ic o", p=128)
        if "P4" in ph:
            self.r_w_out = self.ext_in("r_w_out", [D, D]).rearrange("(kc p) n -> p kc n", p=128)
        if "F0" in ph or "F1" in ph:
            self.ffn_w_in = self.ext_in("ffn_w_in", [2, D, 2 * DFF])
            self.ffn_w_out = self.ext_in("ffn_w_out", [2, DFF, D])
        self.vecs_d = self.ext_in("vecs", [128, NV])
        self.consts_d = self.ext_in("consts", [128, 3, 128])
        self.inc_d = self.ext_in("inc", [128, 8])
        self.sel_d = self.ext_in("sel", [128, 8])
        self.S1 = self.handoff("S1", [128, SW], "P1", ["X1"])
        self.R = self.handoff("R", [128, KC, tok], "P2", ["F0", "P3", "P4", "F1"])
        self.RL3 = self.handoff("RL3", [128, 48], "F0", ["X2"])
        self.YL = self.handoff("YL", [128, KC, tok], "P3", ["P4"])
        self.QS = self.handoff("QS", [128, KC, tok], "P3", ["P4"])
        self.S3 = self.handoff("S3", [128, 32], "P3", ["X3"])
        self.R2 = self.handoff("R2", [128, KC, tok], "P4", ["F1"])
        self.HN0 = self.handoff("HN0", [128, KC, tok], "P2", ["F0"], dtype=BF16)
        self.HN1 = self.handoff("HN1", [128, KC, tok], "P4", ["F1"], dtype=BF16)
        fused = self.launch == "ALL"
        self.fused = fused

        def xall(name, n, consumer):
            if fused:
                return nc.dram_tensor(name, [NCORES * 128, n], F32).ap()
            if consumer in ph:
                return self.ext_in(name, [NCORES * 128, n])
            return None
        self.S1all = xall("S1all", SW, "P2")
        self.RL3all = xall("RL3all", 48, "P3")
        self.S3all = xall("S3all", 32, "P4")
        if "F1" in ph:
            self.OUT = nc.dram_tensor("outT", [128, KC, tok], F32, kind="ExternalOutput").ap()
            self.outnames.append("outT")
        self.vecs = P.sbuf("vecs_sb", [128, NV], F32)
        self.cst = P.sbuf("cst_f", [128, 3, 128], F32)
        self.cstb = P.sbuf("cst_b", [128, 3, 128], BF16)
        self.inc = P.sbuf("inc_sb", [128, 8], F32)
        self.sel = P.sbuf("sel_sb", [128, 8], F32)
        self.arena = P.sbuf("arena", [128, ARENA], F32)
        self.PB = [P.psum("pb%d" % i, [128, 512]) for i in range(7)]
        self.PT = P.psum("pt", [128, 8, 128], BF16)
        rc = Res()
        self.rc = rc
        P.dma("sp", self.vecs[:], self.vecs_d, writes=[rc])
        P.dma("sp", self.cst[:], self.consts_d, writes=[rc])
        P.dma("sp", self.inc[:], self.inc_d, writes=[rc])
        P.dma("sp", self.sel[:], self.sel_d, writes=[rc])
        self.cp("dve", self.cstb[:], self.cst[:], reads=[rc], writes=[rc])
        self.ones_b = self.cstb[:, 0, :]
        self.ident_b = self.cstb[:, 1, :]
        self.tri_b = self.cstb[:, 2, :]
        self.mask_f = self.cst[:, 2, :]
        P.barrier()
        self.final = []
        for p in ph:
            getattr(self, "ph_" + p)()
            P.barrier()
            if fused and p in ("P1", "F0", "P3"):
                src, dst = {"P1": (self.S1, self.S1all), "F0": (self.RL3, self.RL3all), "P3": (self.S3, self.S3all)}[p]
                P.coll(lambda e, src=src, dst=dst: e.collective_compute("AllGather", ALU.bypass, replica_groups=[list(range(NCORES))],
                                                                        ins=[src.opt()], outs=[dst.opt()]))
                P.barrier()
        P.emit(final_waits=self.final)
        P.close()

    def new_banks(self):
        self.rPB = [Res() for _ in range(7)]
        self.rPT = Res()

    def new_wt(self, n):
        self.WT = [self.abf([128, KC, 512]) for _ in range(n)]
        self.rWT = [Res() for _ in range(n)]
        self.wt_i = 0

    def norm_tile(self, src_ap, src_res, gcol, Rst, rRst, sq, rsq, rstd, rrstd, bank, hn_out, rhn, extra_w_rst=(), extra_w_sq=()):
        P = self.P
        if src_ap is not None:
            P.dma("sp", Rst, src_ap, reads=[src_res] if src_res is not None else [], writes=[rRst] + list(extra_w_rst))
        self.act(sq, Rst, AF.Square, reads=[rRst], writes=[rsq] + list(extra_w_sq))
        pb, rpb = self.PB[bank], self.rPB[bank]
        for dc in range(KC):
            self.mm(pb[:], self.ones_b, sq[:, dc, :], dc == 0, dc == KC - 1, reads=[rsq, self.rc],
                    writes=[rpb] if dc in (0, KC - 1) else [])
        self.ts("dve", rstd, pb[:], 1.0 / D, EPS, ALU.mult, ALU.add, reads=[rpb], writes=[rrstd])
        self.act(rstd, rstd, AF.Sqrt, reads=[rrstd], writes=[rrstd])
        P.op("dve", lambda e: e.reciprocal(out=rstd, in_=rstd), reads=[rrstd], writes=[rrstd])
        for dc in range(KC):
            eng = "dve"
            self.stt(eng, hn_out[:, dc, :], Rst[:, dc, :], self.vecs[:, gcol + dc:gcol + dc + 1], rstd,
                     ALU.mult, ALU.mult, reads=[rRst, rrstd, self.rc], writes=[rhn])

    def mlstm_alloc(self, full):
        self.areset()
        self.new_banks()
        a = self
        a.Rst = a.af32([128, KC, 512]); a.rRst = Res()
        a.hn = a.abf([128, KC, 512]); a.rhn = Res()
        a.sq = a.abf([128, KC, 512]); a.rsq = Res()
        a.rstd = a.af32([128, 512]); a.rrstd = Res()
        a.new_wt(2)
        a.wif = a.abf([128, KC, 16]); a.rwif = Res()
        a.bif = a.af32([128, 16]); a.rbif = Res()
        a.kw = a.abf([128, 4, 1024]); a.rkw = Res()
        a.vtm = a.abf([128, 4, 2048]); a.rvtm = Res()
        a.Cf = a.af32([128, 8, 256]); a.rC = Res()
        a.Cb = a.abf([128, 8, 256]); a.rCb = Res()
        a.nf = a.af32([128, 8]); a.nbb = a.abf([128, 8])
        a.nFacc = a.af32([128, 8]); a.rnF = Res()
        a.z = a.af32([128, 4, 16]); a.nl = a.af32([128, 4, 8]); a.r1 = a.af32([128, 4, 8])
        a.hml = a.abf([128, 3, 32]); a.t1 = a.af32([128, 4, 8]); a.t2 = a.af32([128, 4, 8])
        a.ieb = a.af32([128, 4, 8])
        a.cs = a.af32([128, 4, 8]); a.eb = a.af32([128, 4, 8]); a.wS = a.af32([128, 4, 8]); a.eF = a.af32([128, 4, 8])
        a.rg = Res()
        if full:
            a.qT = a.abf([128, 8, 512]); a.rqT = Res()
            a.kT = a.abf([128, 8, 512]); a.rkT = Res()
            a.sgo = a.af32([128, 4, 2048]); a.rsgo = Res()
            a.hfT = a.sq; a.rhfT = a.rsq
            a.STs = [a.abf([128, 4, 128]) for _ in range(2)]; a.rSTs = [Res(), Res()]
            a.hfin = [a.abf([128, 4, 256]) for _ in range(2)]; a.rhfin = [Res(), Res()]
            a.junk = a.af32([128, 256])
            a.sm = [a.af32([128, 8, 4]) for _ in range(2)]; a.rsm = [Res(), Res()]

    def mlstm_init(self, full):
        P = self.P
        a = self
        P.dma("pool", a.wif, self.m_w_in[:, :, 6144:6160], writes=[a.rwif])
        P.dma("sp", a.bif, self.m_b_if.partition_broadcast(128), writes=[a.rbif])
        P.op("dve", lambda e: e.memset(a.Cf, 0.0), writes=[a.rC])
        P.op("dve", lambda e: e.memset(a.nf, 0.0), writes=[a.rC])
        P.op("dve", lambda e: e.memset(a.nFacc, 0.0), writes=[a.rnF])
        P.op("pool", lambda e: e.memset(a.Cb, 0.0), writes=[a.rCb])
        P.op("pool", lambda e: e.memset(a.nbb, 0.0), writes=[a.rCb])

    def mlstm_gates(self, t):
        self.mlstm_gates_a(t)
        self.mlstm_gates_b(t)

    def mlstm_gates_a(self, t):
        P = self.P
        a = self
        pg, rpg = self.PB[0], self.rPB[0]
        for c in range(4):
            for kc in range(KC):
                self.mm(pg[:, c * 16:(c + 1) * 16], a.hn[:, kc, c * 128:(c + 1) * 128], a.wif[:, kc, :], kc == 0, kc == KC - 1,
                        reads=[a.rhn, a.rwif], writes=[rpg] if (c, kc) in ((0, 0), (3, KC - 1)) else [])
        pg3 = pg[:, 0:64].rearrange("p (c g) -> p c g", c=4)
        self.tt("dve", a.z, pg3, a.bif.unsqueeze(1).to_broadcast([128, 4, 16]), ALU.add, reads=[rpg, a.rbif], writes=[a.rg])
        ig = a.z[:, :, 0:8]
        zf = a.z[:, :, 8:16]
        self.act(a.nl, zf, AF.Exp, reads=[a.rg], writes=[a.rg], scale=-1.0)
        self.act(a.nl, a.nl, AF.Ln, reads=[a.rg], writes=[a.rg], bias=1.0)
        hm = a.hml
        nl2 = a.nl.rearrange("p c h -> p (c h)")
        r12 = a.r1.rearrange("p c h -> p (c h)")
        self.cp("dve", hm[:, 0, :], nl2, reads=[a.rg], writes=[a.rg])
        self.tt("dve", r12, nl2, hm[:, 0, :], ALU.subtract, reads=[a.rg], writes=[a.rg])
        self.cp("dve", hm[:, 1, :], r12, reads=[a.rg], writes=[a.rg])
        self.tt("dve", r12, r12, hm[:, 1, :], ALU.subtract, reads=[a.rg], writes=[a.rg])
        self.cp("dve", hm[:, 2, :], r12, reads=[a.rg], writes=[a.rg])

    def mlstm_gates_b(self, t):
        a = self
        hm = a.hml
        ig = a.z[:, :, 0:8]
        pc, rpc = self.PB[1], self.rPB[1]
        for j in range(3):
            self.mm(pc[:, 0:32], self.tri_b, hm[:, j, :], j == 0, j == 2, reads=[a.rg, self.rc], writes=[rpc] if j == 0 else [])
        for j in range(3):
            self.mm(pc[:, 32:64], self.ones_b, hm[:, j, :], j == 0, j == 2, reads=[a.rg, self.rc], writes=[rpc] if j == 2 else [])
        nb = pc[:, 0:32].rearrange("p (c h) -> p c h", c=4)
        ntot = pc[:, 32:64].rearrange("p (c h) -> p c h", c=4)
        self.tt("dve", a.t1, ig, nb, ALU.add, reads=[a.rg, rpc], writes=[a.rg])
        self.act(a.cs, a.t1, AF.Exp, reads=[a.rg], writes=[a.rg])
        self.act(a.eb, nb, AF.Exp, reads=[rpc], writes=[a.rg], scale=-1.0)
        self.act(a.ieb, nb, AF.Exp, reads=[rpc], writes=[a.rg])
        self.tt("dve", a.t2, a.t1, ntot, ALU.subtract, reads=[a.rg, rpc], writes=[a.rg])
        self.act(a.wS, a.t2, AF.Exp, reads=[a.rg], writes=[a.rg], bias=float(-0.5 * np.log(128.0)))
        self.act(a.eF, ntot, AF.Exp, reads=[rpc], writes=[a.rg], scale=-1.0)
        self.tt("dve", a.nFacc, a.nFacc, ntot[:, 0, :], ALU.add, reads=[rpc, a.rnF], writes=[a.rnF])
        for c in range(1, 4):
            self.tt("dve", a.nFacc, a.nFacc, ntot[:, c, :], ALU.add, reads=[rpc, a.rnF], writes=[a.rnF])

    def mlstm_k(self, t):
        a = self
        bi = 2
        for half in range(2):
            w, rw = self.load_w(self.m_w_in, 1024 + half * 512, 512)
            for c in range(4):
                pb, rpb = self.PB[bi], self.rPB[bi]
                bi = 2 + (bi - 2 + 1) % 4
                for kc in range(KC):
                    self.mm(pb[:], a.hn[:, kc, c * 128:(c + 1) * 128], w[:, kc, :], kc == 0, kc == KC - 1,
                            reads=[a.rhn, rw], writes=[rpb] if kc in (0, KC - 1) else [])
                self.tt("dve", a.kw[:, c, half * 512:(half + 1) * 512].rearrange("p (h d) -> p h d", h=4),
                        pb[:].rearrange("p (h d) -> p h d", h=4),
                        a.wS[:, c, half * 4:(half + 1) * 4].unsqueeze(2).to_broadcast([128, 4, 128]), ALU.mult,
                        reads=[rpb, a.rg], writes=[a.rkw])

    def mlstm_v(self, t):
        a = self
        bi = 2
        for vt in range(4):
            w, rw = self.load_w(self.m_w_in, 2048 + vt * 512, 512)
            for c in range(4):
                pb, rpb = self.PB[bi], self.rPB[bi]
                bi = 2 + (bi - 2 + 1) % 4
                for kc in range(KC):
                    self.mm(pb[:], a.hn[:, kc, c * 128:(c + 1) * 128], w[:, kc, :], kc == 0, kc == KC - 1,
                            reads=[a.rhn, rw], writes=[rpb] if kc in (0, KC - 1) else [])
                self.cp("act", a.vtm[:, c, vt * 512:(vt + 1) * 512], pb[:], reads=[rpb], writes=[a.rvtm])

    def mlstm_state_update(self, c, hg):
        a = self
        P = self.P
        pa, rpa = self.PB[5], self.rPB[5]
        pbk, rpbk = self.PB[6], self.rPB[6]
        psm, rpsm = self.PB[1], self.rPB[1]
        for hh in range(4):
            h = hg * 4 + hh
            bank, rb = (pa, rpa) if hh < 2 else (pbk, rpbk)
            col = (hh % 2) * 256
            self.mm(bank[:, col:col + 256], a.kw[:, c, h * 128:(h + 1) * 128], a.vtm[:, c, h * 256:(h + 1) * 256], True, True,
                    reads=[a.rkw, a.rvtm], writes=[rb])
            self.mm(psm[:, 64 + hh:65 + hh], a.kw[:, c, h * 128:(h + 1) * 128], self.ones_b[:, 0:1], True, True,
                    reads=[a.rkw, self.rc], writes=[rpsm] if hh in (0, 3) else [])
        for hh in range(4):
            h = hg * 4 + hh
            bank, rb = (pa, rpa) if hh < 2 else (pbk, rpbk)
            col = (hh % 2) * 256
            self.stt("dve", a.Cf[:, h, :], a.Cf[:, h, :], a.eF[:, c, h:h + 1], bank[:, col:col + 256], ALU.mult, ALU.add,
                     reads=[rb, a.rg, a.rC, a.rCb], writes=[a.rC])
        self.tt("dve", a.nf[:, hg * 4:hg * 4 + 4], a.nf[:, hg * 4:hg * 4 + 4], a.eF[:, c, hg * 4:hg * 4 + 4], ALU.mult,
                reads=[a.rg, a.rC], writes=[a.rC])
        self.tt("dve", a.nf[:, hg * 4:hg * 4 + 4], a.nf[:, hg * 4:hg * 4 + 4], psm[:, 64:68], ALU.add,
                reads=[rpsm, a.rC], writes=[a.rC])
        self.cp("act", a.Cb[:, hg * 4:hg * 4 + 4, :], a.Cf[:, hg * 4:hg * 4 + 4, :], reads=[a.rC], writes=[a.rCb])
        self.cp("act", a.nbb[:, hg * 4:hg * 4 + 4], a.nf[:, hg * 4:hg * 4 + 4], reads=[a.rC], writes=[a.rCb])

    def ph_P1(self):
        P = self.P
        a = self
        self.mlstm_alloc(False)
        self.mlstm_init(False)
        for t in range(self.nt):
            self.norm_tile(self.xT[:, :, t * 512:(t + 1) * 512], None, V_NM0, a.Rst, a.rRst, a.sq, a.rsq, a.rstd, a.rrstd, 0, a.hn, a.rhn)
            self.mlstm_gates_a(t)
            self.mlstm_v(t)
            self.mlstm_gates_b(t)
            self.mlstm_k(t)
            for c in range(4):
                for hg in range(2):
                    self.mlstm_state_update(c, hg)
        o = P.dma("sp", self.S1[:, 0:2048], a.Cf.rearrange("p h v -> p (h v)"), reads=[a.rC])
        o2 = P.dma("sp", self.S1[:, 2048:2056], a.nf, reads=[a.rC])
        o3 = P.dma("sp", self.S1[:, 2056:2064], a.nFacc, reads=[a.rnF])
        self.final += [o, o2, o3]

    def ph_P2(self):
        P = self.P
        a = self
        self.mlstm_alloc(True)
        self.mlstm_init(True)
        stg = a.sgo.rearrange("p c v -> p (c v)")
        rst = Res()
        dec = a.sm[0].rearrange("p a b -> p (a b)")[:, 0:8]
        rdec = Res()
        for cc in range(NCORES):
            sl = stg[:, (cc % 2) * SW:(cc % 2) * SW + SW]
            rs = Res()
            P.dma("sp", sl, self.S1all[cc * 128:(cc + 1) * 128, :], writes=[rs, rst])
            self.ts("dve", dec, sl[:, 2056:2064], self.inc[:, cc:cc + 1], None, ALU.mult, ALU.bypass, reads=[rs, self.rc, rst], writes=[rdec])
            self.act(dec, dec, AF.Exp, reads=[rdec], writes=[rdec], scale=-1.0)
            for h in range(8):
                self.ts("dve", a.Cf[:, h, :], a.Cf[:, h, :], dec[:, h:h + 1], None, ALU.mult, ALU.bypass, reads=[rdec, a.rC], writes=[a.rC])
                self.stt("dve", a.Cf[:, h, :], sl[:, h * 256:(h + 1) * 256], self.inc[:, cc:cc + 1], a.Cf[:, h, :], ALU.mult, ALU.add,
                         reads=[rs, a.rC], writes=[a.rC])
            self.tt("dve", a.nf, a.nf, dec, ALU.mult, reads=[rdec, a.rC], writes=[a.rC])
            self.stt("dve", a.nf, sl[:, 2048:2056], self.inc[:, cc:cc + 1], a.nf, ALU.mult, ALU.add, reads=[rs, a.rC, rst], writes=[a.rC, rst])
        self.cp("act", a.Cb, a.Cf, reads=[a.rC], writes=[a.rCb])
        self.cp("act", a.nbb, a.nf, reads=[a.rC], writes=[a.rCb])
        P.barrier()
        for t in range(self.nt):
            tsl = slice(t * 512, (t + 1) * 512)
            self.norm_tile(self.xT[:, :, tsl], None, V_NM0, a.Rst, a.rRst, a.sq, a.rsq, a.rstd, a.rrstd, 0, a.hn, a.rhn)
            self.mlstm_gates_a(t)
            self.mlstm_v(t)
            self.mlstm_gates_b(t)
            bi = 2
            for qk in range(2):
                for half in range(2):
                    w, rw = self.load_w(self.m_w_in, qk * 1024 + half * 512, 512)
                    for hh in range(4):
                        h = half * 4 + hh
                        pb, rpb = self.PB[bi], self.rPB[bi]
                        bi = 2 + (bi - 2 + 1) % 4
                        for kc in range(KC):
                            self.mm(pb[:], w[:, kc, hh * 128:(hh + 1) * 128], a.hn[:, kc, :], kc == 0, kc == KC - 1,
                                    reads=[a.rhn, rw], writes=[rpb] if kc in (0, KC - 1) else [])
                        if qk == 0:
                            self.cp("act", a.qT[:, h, :], pb[:], reads=[rpb], writes=[a.rqT])
                        else:
                            self.act(a.kT[:, h, :], pb[:], AF.Copy, reads=[rpb], writes=[a.rkT], scale=float(128.0 ** -0.5))
            self.mlstm_k(t)
            for vt in range(4):
                w, rw = self.load_w(self.m_w_in, 4096 + vt * 512, 512)
                for c in range(4):
                    pb, rpb = self.PB[bi], self.rPB[bi]
                    bi = 2 + (bi - 2 + 1) % 4
                    for kc in range(KC):
                        self.mm(pb[:], a.hn[:, kc, c * 128:(c + 1) * 128], w[:, kc, :], kc == 0, kc == KC - 1,
                                reads=[a.rhn, rw], writes=[rpb] if kc in (0, KC - 1) else [])
                    self.act(a.sgo[:, c, vt * 512:(vt + 1) * 512], pb[:], AF.Sigmoid, reads=[rpb], writes=[a.rsgo])
            for c in range(4):
                csl = slice(c * 128, (c + 1) * 128)
                for hg in range(2):
                    pi = hg
                    ps_s, rps_s = self.PB[0], self.rPB[0]
                    pn = [self.PB[2], self.PB[3]]
                    rpn = [self.rPB[2], self.rPB[3]]
                    psm, rpsm = self.PB[1], self.rPB[1]
                    ST, rST = a.STs[pi], a.rSTs[pi]
                    hf, rhf = a.hfin[pi], a.rhfin[pi]
                    sm, rsm = a.sm[pi], a.rsm[pi]
                    for hh in range(4):
                        h = hg * 4 + hh
                        self.mm(ps_s[:, hh * 128:(hh + 1) * 128], a.kT[:, h, csl], a.qT[:, h, csl], True, True,
                                reads=[a.rkT, a.rqT], writes=[rps_s] if hh in (0, 3) else [])
                    for hh in range(4):
                        h = hg * 4 + hh
                        self.stt("dve", ST[:, hh, :], ps_s[:, hh * 128:(hh + 1) * 128], a.cs[:, c, h:h + 1], self.mask_f, ALU.mult, ALU.mult,
                                 reads=[rps_s, a.rg, self.rc], writes=[rST])
                    for hh in range(4):
                        h = hg * 4 + hh
                        bank, rb = pn[hh // 2], rpn[hh // 2]
                        col = (hh % 2) * 256
                        self.mm(bank[:, col:col + 256], ST[:, hh, :], a.vtm[:, c, h * 256:(h + 1) * 256], True, False,
                                reads=[rST, a.rvtm], writes=[rb] if hh % 2 == 0 else [])
                        self.mm(bank[:, col:col + 256], a.qT[:, h, csl], a.Cb[:, h, :], False, True,
                                reads=[a.rqT, a.rCb], writes=[rb] if hh % 2 == 1 else [])
                        self.mm(psm[:, hh:hh + 1], ST[:, hh, :], self.ones_b[:, 0:1], True, False,
                                reads=[rST, self.rc], writes=[rpsm] if hh == 0 else [])
                        self.mm(psm[:, hh:hh + 1], a.qT[:, h, csl], a.nbb[:, h:h + 1], False, True,
                                reads=[a.rqT, a.rCb], writes=[rpsm] if hh == 3 else [])
                    ebs = a.eb[:, c, hg * 4:hg * 4 + 4]
                    d1, d2, rr, ss, sq2, rs2, tot = (sm[:, i, :] for i in range(7))
                    iebs = a.ieb[:, c, hg * 4:hg * 4 + 4]
                    self.act(d2, psm[:, 0:4], AF.Abs, reads=[rpsm], writes=[rsm])
                    self.tt("dve", d2, d2, iebs, ALU.max, reads=[rsm, a.rg], writes=[rsm])
                    P.op("dve", lambda e, ss=ss, d2=d2: e.reciprocal(out=ss, in_=d2), reads=[rsm], writes=[rsm])
                    for hh in range(4):
                        bank, rb = pn[hh // 2], rpn[hh // 2]
                        col = (hh % 2) * 256
                        self.act(a.junk, bank[:, col:col + 256], AF.Square, reads=[rb, rsm], writes=[rsm],
                                 scale=ss[:, hh:hh + 1], accum_out=sq2[:, hh:hh + 1])
                    self.ts("dve", rs2, sq2, 1.0 / 256.0, EPS, ALU.mult, ALU.add, reads=[rsm], writes=[rsm])
                    self.act(rs2, rs2, AF.Sqrt, reads=[rsm], writes=[rsm])
                    P.op("dve", lambda e, rs2=rs2: e.reciprocal(out=rs2, in_=rs2), reads=[rsm], writes=[rsm])
                    self.tt("dve", tot, ss, rs2, ALU.mult, reads=[rsm], writes=[rsm])
                    for hh in range(4):
                        h = hg * 4 + hh
                        bank, rb = pn[hh // 2], rpn[hh // 2]
                        col = (hh % 2) * 256
                        self.stt("dve", hf[:, hh, :], bank[:, col:col + 256], tot[:, hh:hh + 1], a.sgo[:, c, h * 256:(h + 1) * 256],
                                 ALU.mult, ALU.mult, reads=[rb, rsm, a.rsgo], writes=[rhf])
                    for j in range(8):
                        self.P.op("pe", lambda e, j=j, hf=hf: e.transpose(out=self.PT[:, j, :], in_=hf[:, j // 2, (j % 2) * 128:(j % 2 + 1) * 128],
                                                                       identity=self.ident_b),
                                  reads=[rhf, self.rc], writes=[self.rPT] if j in (0, 7) else [])
                    self.tt("dve", a.hfT[:, hg * 8:hg * 8 + 8, csl], self.PT[:],
                            self.vecs[:, V_HNG + hg * 8:V_HNG + hg * 8 + 8].unsqueeze(2).to_broadcast([128, 8, 128]), ALU.mult,
                            reads=[self.rPT, self.rc], writes=[a.rhfT])
                    self.mlstm_state_update(c, hg)
            for q4 in range(4):
                w, rw = self.load_w(self.m_w_out, q4 * 512, 512)
                for dd in range(4):
                    dc = q4 * 4 + dd
                    pb, rpb = self.PB[bi], self.rPB[bi]
                    bi = 2 + (bi - 2 + 1) % 4
                    for kc in range(KC):
                        self.mm(pb[:], w[:, kc, dd * 128:(dd + 1) * 128], a.hfT[:, kc, :], kc == 0, kc == KC - 1,
                                reads=[a.rhfT, rw], writes=[rpb] if kc in (0, KC - 1) else [])
                    self.tt("dve", a.Rst[:, dc, :], a.Rst[:, dc, :], pb[:], ALU.add, reads=[rpb, a.rRst], writes=[a.rRst])
            P.dma("sp", self.R[:, :, tsl], a.Rst, reads=[a.rRst])
            self.norm_tile(None, None, V_NF0, a.Rst, a.rRst, a.sq, a.rsq, a.rstd, a.rrstd, 0, a.hn, a.rhn)
            P.dma("sp", self.HN0[:, :, tsl], a.hn, reads=[a.rhn])


    def ph_F0(self):
        self.ffn(0, self.R, self.HN0)

    def ph_F1(self):
        self.ffn(1, self.R2, self.HN1)
        self.final_norm()

    def ffn(self, l, R, HN=None):
        P, tok, nt = self.P, self.tok, self.nt
        self.areset()
        self.new_banks()
        hn = self.abf([128, KC, tok]); rhn = Res()
        actT = self.abf([128, GRP, tok]); ract = Res()
        sq = actT.rearrange("p g t -> p (g t)")[:, 0:KC * 512].rearrange("p (a b) -> p a b", a=KC)
        Rst = self.af32([128, KC, 512]); rRst = Res()
        rstd = self.af32([128, 512]); rrstd = Res()
        wgu = [self.abf([128, KC, 256]) for _ in range(2)]; rwgu = [Res(), Res()]
        wo = [self.abf([128, GRP, 128]) for _ in range(2)]; rwo = [Res(), Res()]
        rt = [self.af32([128, tok]) for _ in range(3)]; rrt = [Res(), Res(), Res()]
        sg = [self.af32([128, 512]) for _ in range(2)]; rsg = [Res(), Res()]
        gcol = V_NF0 if l == 0 else V_NF1
        if HN is not None:
            for t in range(nt):
                tsl = slice(t * 512, (t + 1) * 512)
                P.dma("sp", hn[:, :, tsl], HN[:, :, tsl], writes=[rhn])
        else:
            for t in range(nt):
                tsl = slice(t * 512, (t + 1) * 512)
                self.norm_tile(R[:, :, tsl], None, gcol, Rst, rRst, sq, ract, rstd, rrstd, 6, hn[:, :, tsl], rhn)
        win = self.ffn_w_in[l].rearrange("(kc p) n -> p kc n", p=128)
        wout = self.ffn_w_out[l].rearrange("(fc p) n -> p fc n", p=128)
        rR = [Res() for _ in range(KC)]
        cnt = 0
        cnt2 = 0
        for g in range(NGRP):
            for fi in range(GRP):
                f = g * GRP + fi
                wb, rwb = wgu[f % 2], rwgu[f % 2]
                P.dma("pool", wb[:, :, 0:128], win[:, :, f * 128:(f + 1) * 128], writes=[rwb])
                P.dma("pool", wb[:, :, 128:256], win[:, :, DFF + f * 128:DFF + (f + 1) * 128], writes=[rwb])
                for t in range(nt):
                    tsl = slice(t * 512, (t + 1) * 512)
                    pr = cnt % 2
                    cnt += 1
                    pg, rpg = self.PB[2 * pr], self.rPB[2 * pr]
                    pu, rpu = self.PB[2 * pr + 1], self.rPB[2 * pr + 1]
                    for kc in range(KC):
                        self.mm(pg[:], wb[:, kc, 0:128], hn[:, kc, tsl], kc == 0, kc == KC - 1, reads=[rwb, rhn],
                                writes=[rpg] if kc in (0, KC - 1) else [])
                    for kc in range(KC):
                        self.mm(pu[:], wb[:, kc, 128:256], hn[:, kc, tsl], kc == 0, kc == KC - 1, reads=[rwb, rhn],
                                writes=[rpu] if kc in (0, KC - 1) else [])
                    self.act(sg[pr], pg[:], AF.Silu, reads=[rpg], writes=[rsg[pr]])
                    self.tt("dve", actT[:, fi, tsl], sg[pr], pu[:], ALU.mult, reads=[rsg[pr], rpu], writes=[ract])
            for dc in range(KC):
                wob, rwob = wo[dc % 2], rwo[dc % 2]
                P.dma("pool", wob, wout[:, g * GRP:(g + 1) * GRP, dc * 128:(dc + 1) * 128], writes=[rwob])
                rtb, rr = rt[dc % 3], rrt[dc % 3]
                if dc == 0:
                    for d2 in range(2):
                        P.dma("sp", rt[d2 % 3], R[:, d2, :], reads=[rR[d2]], writes=[rrt[d2 % 3]])
                if dc + 2 < KC:
                    P.dma("sp", rt[(dc + 2) % 3], R[:, dc + 2, :], reads=[rR[dc + 2]], writes=[rrt[(dc + 2) % 3]])
                for t in range(nt):
                    tsl = slice(t * 512, (t + 1) * 512)
                    po, rpo = self.PB[4 + cnt2 % 2], self.rPB[4 + cnt2 % 2]
                    cnt2 += 1
                    for fi in range(GRP):
                        self.mm(po[:], wob[:, fi, :], actT[:, fi, tsl], fi == 0, fi == GRP - 1, reads=[rwob, ract],
                                writes=[rpo] if fi in (0, GRP - 1) else [])
                    self.tt("dve", rtb[:, tsl], rtb[:, tsl], po[:], ALU.add, reads=[rpo, rr], writes=[rr])
                P.dma("sp", R[:, dc, :], rtb, reads=[rr], writes=[rR[dc]])
                if g == NGRP - 1 and l == 0:
                    P.dma("sp", self.RL3[:, dc * 3:(dc + 1) * 3], rtb[:, tok - 3:tok], reads=[rr])

    def final_norm(self):
        P = self.P
        P.barrier()
        self.areset()
        self.new_banks()
        Rst = self.af32([128, KC, 512]); rRst = Res()
        sq = self.abf([128, KC, 512]); rsq = Res()
        rstd = self.af32([128, 512]); rrstd = Res()
        ob = [self.af32([128, KC, 512]) for _ in range(2)]; rob = [Res(), Res()]
        Rst2 = self.af32([128, KC, 512]); rRst2 = Res()
        sq2 = self.abf([128, KC, 512]); rsq2 = Res()
        rstd2 = self.af32([128, 512]); rrstd2 = Res()
        RB = [(Rst, rRst, sq, rsq, rstd, rrstd, 6), (Rst2, rRst2, sq2, rsq2, rstd2, rrstd2, 5)]
        for t in range(min(2, self.nt)):
            P.dma("sp", RB[t][0], self.R2[:, :, t * 512:(t + 1) * 512], writes=[RB[t][1]])
        self._final_tiles(RB, ob, rob)

    def _final_tiles(self, RB, ob, rob):
        P = self.P
        for t in range(self.nt):
            tsl = slice(t * 512, (t + 1) * 512)
            Rb, rRb, sqb, rsqb, rstdb, rrstdb, bank = RB[t % 2]
            self.norm_tile(None, None, V_NFIN, Rb, rRb, sqb, rsqb, rstdb, rrstdb, bank, ob[t % 2], rob[t % 2])
            P.dma("sp", self.OUT[:, :, tsl], ob[t % 2], reads=[rob[t % 2]])
            if t + 2 < self.nt:
                t2 = t + 2
                P.dma("sp", Rb, self.R2[:, :, t2 * 512:(t2 + 1) * 512], writes=[rRb])

    def ph_P3(self):
        P, tok, nt = self.P, self.tok, self.nt
        self.areset()
        self.new_banks()
        Rst = self.af32([128, KC, 512]); rRst = Res()
        rc_, rrc = Rst, rRst
        hn = self.abf([128, KC, 512]); rhn = Res()
        sq = self.abf([128, KC, 512]); rsq = Res()
        rcb, rrcb = sq, rsq
        rstd = self.af32([128, 512]); rrstd = Res()
        self.WT = [self.abf([128, KC, 256]) for _ in range(2)]; self.rWT = [Res(), Res()]; self.wt_i = 0
        GG = self.af32([128, KC, 512]); rGG = Res()
        recx = self.af32([128, KC, 516]); rrx = Res()
        gw = self.abf([128, 16, 512]); rgw = Res()
        tmps = [[self.af32([128, 512]) for _ in range(7)] for _ in range(2)]
        rtm = [Res(), Res()]
        zeros = self.af32([128, 512])
        hst = self.af32([128, 16]); pst = self.af32([128, 16]); rstt = Res()
        c1 = self.af32([128, 16]); cu = self.af32([128, 16]); cd = self.af32([128, 16]); rc1 = Res()
        h3 = self.af32([128, KC, 4]); hn3 = self.abf([128, KC, 4]); sq3 = self.abf([128, KC, 4]); rstd3 = self.af32([128, 4])
        ld3 = [self.af32([128, 48]) for _ in range(2)]
        r3 = Res()
        P.op("dve", lambda e: e.memset(zeros, 0.0), writes=[rtm[0], rtm[1]])
        P.op("dve", lambda e: e.memset(hst, 0.0), writes=[rstt])
        P.op("dve", lambda e: e.memset(pst, 1.0), writes=[rstt])
        P.op("dve", lambda e: e.memset(h3, 0.0), writes=[r3])
        P.op("pool", lambda e: e.memset(recx, 0.0), writes=[rrx])
        P.dma("pool", gw.rearrange("p (g i) o -> p g i o", g=8), self.r_gate_w, writes=[rgw])
        ap_ = self.vecs[:, V_AP:V_AP + 16]
        self.act(c1, ap_, AF.Exp, reads=[self.rc], writes=[rc1], scale=-1.0)
        self.ts("dve", cu, c1, 1.0, None, ALU.add, ALU.bypass, reads=[rc1], writes=[rc1])
        self.ts("dve", cd, cu, -1.0, 1e-30, ALU.add, ALU.max, reads=[rc1], writes=[rc1])
        P.op("dve", lambda e: e.reciprocal(out=cd, in_=cd), reads=[rc1], writes=[rc1])
        self.act(cu, cu, AF.Ln, reads=[rc1], writes=[rc1])
        self.tt("dve", c1, c1, cd, ALU.mult, reads=[rc1], writes=[rc1])
        self.stt("dve", c1, c1, -8.0, cu, ALU.mult, ALU.mult, reads=[rc1], writes=[rc1])
        for cc in range(NCORES):
            lb = ld3[cc % 2]
            rl = Res()
            P.dma("sp", lb, self.RL3all[cc * 128:(cc + 1) * 128, :], reads=[r3], writes=[rl])
            self.stt("dve", h3[:, :, 0:3], lb.rearrange("p (a b) -> p a b", a=KC), self.sel[:, cc:cc + 1], h3[:, :, 0:3], ALU.mult, ALU.add,
                     reads=[rl, r3, self.rc], writes=[r3])
        self.act(sq3, h3, AF.Square, reads=[r3], writes=[r3])
        pb, rpb = self.PB[6], self.rPB[6]
        for dc in range(KC):
            self.mm(pb[:, 0:4], self.ones_b, sq3[:, dc, :], dc == 0, dc == KC - 1, reads=[r3, self.rc], writes=[rpb] if dc in (0, KC - 1) else [])
        self.ts("dve", rstd3, pb[:, 0:4], 1.0 / D, EPS, ALU.mult, ALU.add, reads=[rpb], writes=[r3])
        self.act(rstd3, rstd3, AF.Sqrt, reads=[r3], writes=[r3])
        P.op("dve", lambda e: e.reciprocal(out=rstd3, in_=rstd3), reads=[r3], writes=[r3])
        for dc in range(KC):
            self.stt("dve", hn3[:, dc, :], h3[:, dc, :], self.vecs[:, V_NM1 + dc:V_NM1 + dc + 1], rstd3, ALU.mult, ALU.mult,
                     reads=[r3, self.rc], writes=[r3])
        bi = 2
        for q8 in range(8):
            w, rw = self.load_w(self.r_w_in, D + q8 * 256, 256)
            for dd in range(2):
                oc = q8 * 2 + dd
                pb2, rpb2 = self.PB[bi], self.rPB[bi]
                bi = 2 + (bi - 2 + 1) % 4
                for kc in range(KC):
                    self.mm(pb2[:, 0:4], w[:, kc, dd * 128:(dd + 1) * 128], hn3[:, kc, :], kc == 0, kc == KC - 1, reads=[rw, r3],
                            writes=[rpb2] if kc in (0, KC - 1) else [])
                self.cp("dve", recx[:, oc, 1:4], pb2[:, 0:3], reads=[rpb2], writes=[rrx])
        rrxc = [Res() for _ in range(KC)]
        rrcc = [Res() for _ in range(KC)]
        rrcbc = [Res() for _ in range(KC)]
        rGGc = [Res() for _ in range(KC)]
        rstc = [Res() for _ in range(KC)]
        rtmp = [[Res() for _ in range(6)] for _ in range(2)]
        banks = [0, 1, 2, 3, 4, 5]
        bstate = [0]

        def nb():
            b_ = banks[bstate[0] % len(banks)]
            bstate[0] += 1
            return self.PB[b_], self.rPB[b_]

        for t in range(nt):
            tsl = slice(t * 512, (t + 1) * 512)
            self.norm_tile(self.R[:, :, tsl], None, V_NM1, Rst, rRst, sq, rsq, rstd, rrstd, 6, hn, rhn,
                           extra_w_rst=rrcc, extra_w_sq=rrcbc)
            if t > 0:
                self.cp("pool", recx[:, :, 1:4], recx[:, :, 513:516], reads=rrxc + [rrx], writes=rrxc + [rrx])
            order = list(range(16, 32)) + list(range(0, 16))
            for qi in range(16):
                oc0 = order[qi * 2]
                w, rw = self.load_w(self.r_w_in, oc0 * 128, 256)
                for dd in range(2):
                    oc = oc0 + dd
                    pb2, rpb2 = nb()
                    for kc in range(KC):
                        self.mm(pb2[:], w[:, kc, dd * 128:(dd + 1) * 128], hn[:, kc, :], kc == 0, kc == KC - 1, reads=[rw, rhn],
                                writes=[rpb2] if kc in (0, KC - 1) else [])
                    if oc < 16:
                        self.act(GG[:, oc, :], pb2[:], AF.Gelu, reads=[rpb2], writes=[rGGc[oc]])
                    else:
                        ch = oc - 16
                        self.cp("act", recx[:, ch, 4:516], pb2[:], reads=[rpb2, rrx], writes=[rrxc[ch]])
                        cw = lambda j, ch=ch: self.vecs[:, V_CW + j * 16 + ch:V_CW + j * 16 + ch + 1]
                        self.ts("dve", rc_[:, ch, :], recx[:, ch, 1:513], cw(0), self.vecs[:, V_CB + ch:V_CB + ch + 1], ALU.mult, ALU.add,
                                reads=[rrxc[ch], self.rc], writes=[rrcc[ch], rRst])
                        for j in range(1, 4):
                            self.stt("dve", rc_[:, ch, :], recx[:, ch, 1 + j:513 + j], cw(j), rc_[:, ch, :], ALU.mult, ALU.add,
                                     reads=[rrxc[ch], self.rc, rrcc[ch]], writes=[rrcc[ch]])
                        self.cp("act", rcb[:, ch, :], rc_[:, ch, :], reads=[rrcc[ch]], writes=[rrcbc[ch], rsq])
            for g in range(8):
                pbs = []
                for oc in range(4):
                    pb2, rpb2 = nb()
                    for ic in range(2):
                        self.mm(pb2[:], gw[:, g * 2 + ic, oc * 128:(oc + 1) * 128], rcb[:, 2 * g + ic, :], ic == 0, ic == 1,
                                reads=[rgw, rrcbc[2 * g + ic]], writes=[rpb2])
                    pbs.append((pb2, rpb2))
                chs = [2 * g, 2 * g + 1]
                TT = {ch: (tmps[ch % 2], rtmp[ch % 2]) for ch in chs}
                for oc, ch in enumerate(chs):
                    T, rT = TT[ch]
                    self.act(T[0], pbs[oc][0][:], AF.Sigmoid, reads=[pbs[oc][1], self.rc], writes=[rT[0]], bias=self.vecs[:, V_GBR + ch:V_GBR + ch + 1])
                for oc, ch in enumerate(chs):
                    T, rT = TT[ch]
                    self.act(T[1], pbs[oc + 2][0][:], AF.Sigmoid, reads=[pbs[oc + 2][1], self.rc], writes=[rT[1]], bias=self.vecs[:, V_GBI + ch:V_GBI + ch + 1])
                for ch in chs:
                    T, rT = TT[ch]
                    self.act(T[2], T[0], AF.Exp, reads=[rT[0], rc1], writes=[rT[2]], scale=c1[:, ch:ch + 1])
                for ch in chs:
                    T, rT = TT[ch]
                    self.tt("pool", T[3], T[2], T[2], ALU.mult, reads=[rT[2]], writes=[rT[3]])
                for ch in chs:
                    T, rT = TT[ch]
                    self.tt("dve", T[1], T[1], rc_[:, ch, :], ALU.mult, reads=[rT[1], rrcc[ch]], writes=[rT[1]])
                for ch in chs:
                    T, rT = TT[ch]
                    self.act(T[3], T[3], AF.Sqrt, reads=[rT[3]], writes=[rT[3]], scale=-1.0, bias=1.0)
                for ch in chs:
                    T, rT = TT[ch]
                    P.op("dve", lambda e, T=T, ch=ch: e.tensor_tensor_scan(out=T[5], data0=T[2], data1=zeros, initial=pst[:, ch:ch + 1], op0=ALU.mult, op1=ALU.add),
                         reads=[rT[2], rstc[ch], rstt], writes=[rT[5]])
                for ch in chs:
                    T, rT = TT[ch]
                    self.tt("dve", T[1], T[1], T[3], ALU.mult, reads=[rT[1], rT[3]], writes=[rT[1]])
                for ch in chs:
                    T, rT = TT[ch]
                    P.op("dve", lambda e, T=T, ch=ch: e.tensor_tensor_scan(out=T[4], data0=T[2], data1=T[1], initial=hst[:, ch:ch + 1], op0=ALU.mult, op1=ALU.add),
                         reads=[rT[2], rT[1], rstc[ch], rstt], writes=[rT[4]])
                for ch in chs:
                    T, rT = TT[ch]
                    self.cp("act", pst[:, ch:ch + 1], T[5][:, 511:512], reads=[rT[5]], writes=[rstc[ch]])
                    self.cp("act", hst[:, ch:ch + 1], T[4][:, 511:512], reads=[rT[4]], writes=[rstc[ch]])
                for ch in chs:
                    T, rT = TT[ch]
                    self.tt("pool", T[5], T[5], GG[:, ch, :], ALU.mult, reads=[rT[5], rGGc[ch], rstc[ch]], writes=[rT[5]])
                    self.tt("pool", T[4], T[4], GG[:, ch, :], ALU.mult, reads=[rT[4], rGGc[ch], rstc[ch]], writes=[rT[4]])
                    P.dma("sp", self.QS[:, ch, tsl], T[5], reads=[rT[5]])
                    P.dma("sp", self.YL[:, ch, tsl], T[4], reads=[rT[4]])
        rstt_all = rstc + [rstt]
        P.dma("sp", self.S3[:, 0:16], hst, reads=rstt_all)
        P.dma("sp", self.S3[:, 16:32], pst, reads=rstt_all)

    def ph_P4(self):
        P, tok, nt = self.P, self.tok, self.nt
        self.areset()
        self.new_banks()
        Rst = self.af32([128, KC, 512]); rRst = Res()
        yl = self.af32([128, KC, 512]); ryl = Res()
        qs = self.af32([128, KC, 512]); rqs = Res()
        yT = self.abf([128, KC, 512]); ryT = Res()
        self.WT = [self.abf([128, KC, 256]) for _ in range(2)]; self.rWT = [Res(), Res()]; self.wt_i = 0
        hin = self.af32([128, 16]); dec = self.af32([128, 16]); rh = Res()
        ld = [self.af32([128, 32]) for _ in range(2)]
        P.op("dve", lambda e: e.memset(hin, 0.0), writes=[rh])
        for cc in range(NCORES):
            lb = ld[cc % 2]
            rl = Res()
            P.dma("sp", lb, self.S3all[cc * 128:(cc + 1) * 128, :], reads=[rh], writes=[rl])
            self.ts("dve", dec, lb[:, 16:32], -1.0, self.inc[:, cc:cc + 1], ALU.add, ALU.mult, reads=[rl, self.rc, rh], writes=[rh])
            self.ts("dve", dec, dec, 1.0, None, ALU.add, ALU.bypass, reads=[rh], writes=[rh])
            self.tt("dve", hin, hin, dec, ALU.mult, reads=[rh], writes=[rh])
            self.stt("dve", hin, lb[:, 0:16], self.inc[:, cc:cc + 1], hin, ALU.mult, ALU.add, reads=[rl, rh], writes=[rh])
        bi = 2
        yl2 = self.af32([128, KC, 512]); qs2 = self.af32([128, KC, 512])
        rstd4 = self.af32([128, 512]); rrstd4 = Res()
        RS = [(Rst, rRst), (Rst, rRst)]
        YLb = [(yl, ryl), (yl2, Res())]
        QSb = [(qs, rqs), (qs2, Res())]

        def loads(t):
            tsl_ = slice(t * 512, (t + 1) * 512)
            P.dma("sp", QSb[t % 2][0], self.QS[:, :, tsl_], writes=[QSb[t % 2][1]])
            P.dma("sp", YLb[t % 2][0], self.YL[:, :, tsl_], writes=[YLb[t % 2][1]])

        def loadR(t):
            tsl_ = slice(t * 512, (t + 1) * 512)
            P.dma("sp", Rst, self.R[:, :, tsl_], writes=[rRst])

        loads(0)
        loadR(0)
        for t in range(nt):
            tsl = slice(t * 512, (t + 1) * 512)
            Rb, rRb = RS[t % 2]
            ylb, rylb = YLb[t % 2]
            qsb, rqsb = QSb[t % 2]
            for ch in range(KC):
                self.stt("dve", yT[:, ch, :], qsb[:, ch, :], hin[:, ch:ch + 1], ylb[:, ch, :], ALU.mult, ALU.add, reads=[rqsb, rylb, rh], writes=[ryT])
            if t + 1 < nt:
                loads(t + 1)
            for q8 in range(8):
                w, rw = self.load_w(self.r_w_out, q8 * 256, 256)
                for dd in range(2):
                    dc = q8 * 2 + dd
                    pb, rpb = self.PB[bi], self.rPB[bi]
                    bi = 2 + (bi - 2 + 1) % 4
                    for kc in range(KC):
                        self.mm(pb[:], w[:, kc, dd * 128:(dd + 1) * 128], yT[:, kc, :], kc == 0, kc == KC - 1, reads=[rw, ryT],
                                writes=[rpb] if kc in (0, KC - 1) else [])
                    self.tt("dve", Rb[:, dc, :], Rb[:, dc, :], pb[:], ALU.add, reads=[rpb, rRb], writes=[rRb])
            P.dma("sp", self.R2[:, :, tsl], Rb, reads=[rRb])
            sqv = ylb.rearrange("p a b -> p (a b)")[:, 0:KC * 256].bitcast(BF16).rearrange("p (a b) -> p a b", a=KC)
            qsv = qsb.rearrange("p a b -> p (a b)")[:, 0:KC * 256].bitcast(BF16).rearrange("p (a b) -> p a b", a=KC)
            self.norm_tile(None, None, V_NF1, Rb, rRb, sqv, rylb, rstd4, rrstd4, 6, qsv, rqsb)
            P.dma("sp", self.HN1[:, :, tsl], qsv, reads=[rqsb])
            if t + 1 < nt:
                loadR(t + 1)


_NC_CACHE = {}
_FUSED = True
_DBG_CORES = None
_DBG_RUNNER = None


def _get(launch, tok=2048):
    key = (launch, tok)
    if key not in _NC_CACHE:
        _NC_CACHE[key] = KB(launch, tok)
    return _NC_CACHE[key]


def _pc(v):
    return np.ascontiguousarray(np.asarray(v, np.float32).reshape(-1, 128).T)


def _host_common(inp, tok):
    vecs = np.zeros((128, NV), np.float32)
    vecs[:, V_NM0:V_NM0 + 16] = _pc(inp["norm_mix"][0])
    vecs[:, V_NM1:V_NM1 + 16] = _pc(inp["norm_mix"][1])
    vecs[:, V_NF0:V_NF0 + 16] = _pc(inp["norm_ffn"][0])
    vecs[:, V_NF1:V_NF1 + 16] = _pc(inp["norm_ffn"][1])
    vecs[:, V_NFIN:V_NFIN + 16] = _pc(inp["norm_final"])
    vecs[:, V_HNG:V_HNG + 16] = _pc(inp["m_head_norm"][0])
    for j in range(4):
        vecs[:, V_CW + j * 16:V_CW + (j + 1) * 16] = _pc(inp["r_conv_w"][0, j])
    vecs[:, V_CB:V_CB + 16] = _pc(inp["r_conv_b"][0])
    gb = np.asarray(inp["r_gate_b"][0], np.float32)
    vecs[:, V_GBR:V_GBR + 16] = _pc(gb[:, 0:256].reshape(-1))
    vecs[:, V_GBI:V_GBI + 16] = _pc(gb[:, 256:512].reshape(-1))
    vecs[:, V_AP:V_AP + 16] = _pc(inp["r_a_param"][0])
    consts = np.zeros((128, 3, 128), np.float32)
    consts[:, 0, :] = 1.0
    consts[:, 1, :] = np.eye(128, dtype=np.float32)
    consts[:, 2, :] = np.triu(np.ones((128, 128), np.float32))
    return vecs, consts


def kernel(x, norm_mix, norm_ffn, norm_final, m_w_in, m_b_if, m_head_norm, m_w_out,
           r_w_in, r_conv_w, r_conv_b, r_gate_w, r_gate_b, r_a_param, r_w_out, ffn_w_in, ffn_w_out):
    inp = dict(norm_mix=norm_mix, norm_ffn=norm_ffn, norm_final=norm_final, m_head_norm=m_head_norm,
               r_conv_w=r_conv_w, r_conv_b=r_conv_b, r_gate_b=r_gate_b, r_a_param=r_a_param)
    B, S, _ = x.shape
    tok = 2048
    segs = S // tok
    ncores = B * segs
    assert ncores == NCORES
    vecs, consts = _host_common(inp, tok)
    x = np.asarray(x, np.float32)
    base = []
    for c in range(ncores):
        b, j = divmod(c, segs)
        xs = x[b, j * tok:(j + 1) * tok, :]
        xT = np.ascontiguousarray(xs.T.reshape(KC, 128, tok).transpose(1, 0, 2))
        inc = np.zeros((128, 8), np.float32)
        sel = np.zeros((128, 8), np.float32)
        for cc in range(ncores):
            if cc // segs == b and cc < c:
                inc[:, cc] = 1.0
            if cc // segs == b and cc == c - 1:
                sel[:, cc] = 1.0
        base.append(dict(xT=xT, vecs=vecs, consts=consts, inc=inc, sel=sel))
    W = dict(m_w_in=np.ascontiguousarray(np.asarray(m_w_in, np.float32)[0]), m_b_if=np.ascontiguousarray(np.asarray(m_b_if, np.float32)[0]),
             m_w_out=np.ascontiguousarray(np.asarray(m_w_out, np.float32)[0]), r_w_in=np.ascontiguousarray(np.asarray(r_w_in, np.float32)[0]),
             r_gate_w=np.ascontiguousarray(np.asarray(r_gate_w, np.float32)[0]), r_w_out=np.ascontiguousarray(np.asarray(r_w_out, np.float32)[0]),
             ffn_w_in=np.asarray(ffn_w_in, np.float32), ffn_w_out=np.asarray(ffn_w_out, np.float32))
    cores = list(range(ncores)) if _DBG_CORES is None else list(_DBG_CORES)
    com = ["vecs", "consts", "inc", "sel"]

    def run(launch, names, extra):
        kb = _get(launch, tok)
        maps = []
        for c in cores:
            m = {k: base[c][k] for k in com}
            for k in names:
                m[k] = base[c][k] if k in base[c] else W[k]
            for k, v in extra.items():
                m[k] = v[cores.index(c)] if isinstance(v, list) else v
            maps.append(m)
        if _DBG_RUNNER is not None:
            return _DBG_RUNNER(kb.nc, maps)
        return run_bass_kernel_spmd(kb.nc, maps, core_ids=cores).results

    def stack(rs, key):
        full = [np.zeros_like(rs[0][key]) for _ in range(ncores)]
        for i, c in enumerate(cores):
            full[c] = rs[i][key]
        return np.ascontiguousarray(np.concatenate(full, 0))

    if _FUSED:
        rr = run("ALL", ["xT", "m_w_in", "m_b_if", "m_w_out", "r_w_in", "r_gate_w", "r_w_out", "ffn_w_in", "ffn_w_out"], {})
        out = np.empty((B, S, D), np.float32)
        for i, c in enumerate(cores):
            b, j = divmod(c, segs)
            out[b, j * tok:(j + 1) * tok, :] = rr[i]["outT"].transpose(2, 1, 0).reshape(tok, D)
        return out
    r1 = run("L1", ["xT", "m_w_in", "m_b_if"], {})
    S1all = stack(r1, "S1")
    r2 = run("L2", ["xT", "m_w_in", "m_b_if", "m_w_out", "ffn_w_in", "ffn_w_out"], {"S1all": S1all})
    RL3all = stack(r2, "RL3")
    Rl = [r["R"] for r in r2]
    r3 = run("L3", ["r_w_in", "r_gate_w"], {"RL3all": RL3all, "R": Rl})
    S3all = stack(r3, "S3")
    r4 = run("L4", ["r_w_out", "ffn_w_in", "ffn_w_out"], {"S3all": S3all, "R": Rl, "YL": [r["YL"] for r in r3], "QS": [r["QS"] for r in r3]})
    out = np.empty((B, S, D), np.float32)
    for i, c in enumerate(cores):
        b, j = divmod(c, segs)
        o = r4[i]["outT"]
        out[b, j * tok:(j + 1) * tok, :] = o.transpose(2, 1, 0).reshape(tok, D)
    return out
```
